# Optimizing a Trainium2 kernel written in Bass

```python
import jax, jax.numpy as jnp
from jax import lax
import numpy as np

D_MODEL = 4096
BATCH = 2
SEQ = 4096
DEPTH = 2

CONV_GROUPS = 16
CONV_GROUP_DIM = 128
CONV_W = CONV_GROUPS * CONV_GROUP_DIM
CONV_TAPS = 3
GLA_HEADS = 8
HEAD_K = 128
HEAD_V = 256
KEY_DIM = GLA_HEADS * HEAD_K
VALUE_DIM = GLA_HEADS * HEAD_V
ALPHA_RANK = 16
GATE_TAU = 16.0
GLA_CHUNK = 64
D_FF = 11008
FFN_TAPS = 3
EPS = 1e-6

IN_SIZES = (CONV_W, CONV_W, CONV_W,
            KEY_DIM, KEY_DIM, VALUE_DIM,
            VALUE_DIM, ALPHA_RANK,
            D_MODEL, D_MODEL)
IN_COLS = sum(IN_SIZES)
IN_SPLITS = tuple(int(s) for s in np.cumsum(IN_SIZES)[:-1])

kernel_name = "hybrid_conv_gla_gated_merge_convffn"


def rmsnorm(x, g):
    xf = x.astype(jnp.float32)
    y = xf * lax.rsqrt(jnp.mean(xf * xf, axis=-1, keepdims=True) + EPS)
    return (y * g.astype(jnp.float32)).astype(x.dtype)


def causal_dwconv3(u, w):
    s = u.shape[1]
    up = jnp.pad(u, ((0, 0), (CONV_TAPS - 1, 0), (0, 0)))
    return up[:, :s] * w[0] + up[:, 1:s + 1] * w[1] + up[:, 2:s + 2] * w[2]


def gla_chunked(q, k, v, log_a):
    bsz, s, h, dk = q.shape
    dv = v.shape[-1]
    n = s // GLA_CHUNK

    def to_chunks(t):
        return t.astype(jnp.float32).reshape(bsz, n, GLA_CHUNK, h, t.shape[-1]).transpose(1, 0, 3, 2, 4)

    qc, kc, vc, gc = to_chunks(q), to_chunks(k), to_chunks(v), to_chunks(log_a)
    b = jnp.cumsum(gc, axis=3)
    b_last = b[:, :, :, -1:, :]
    q_in = qc * jnp.exp(b)
    k_in = kc * jnp.exp(-b)
    k_out = kc * jnp.exp(b_last - b)
    mask = jnp.tril(jnp.ones((GLA_CHUNK, GLA_CHUNK), dtype=bool))
    scores = jnp.where(mask, jnp.einsum('nbhid,nbhjd->nbhij', q_in, k_in), 0.0)
    o_intra = jnp.einsum('nbhij,nbhjv->nbhiv', scores, vc)
    decay = jnp.exp(b_last[:, :, :, 0, :])

    def step(state, inp):
        q_c, k_c, v_c, d_c = inp
        o_c = jnp.einsum('bhid,bhdv->bhiv', q_c, state)
        state = d_c[..., None] * state + jnp.einsum('bhjd,bhjv->bhdv', k_c, v_c)
        return state, o_c

    state0 = jnp.zeros((bsz, h, dk, dv), jnp.float32)
    _, o_inter = lax.scan(step, state0, (q_in, k_out, vc, decay))
    o = o_intra + o_inter
    return o.transpose(1, 0, 3, 2, 4).reshape(bsz, s, h, dv).astype(v.dtype)


def setup_inputs(seed: int = 0) -> dict:
    key = jax.random.key(seed)
    ks = jax.random.split(key, 20)
    L, D = DEPTH, D_MODEL

    def nrm(k, shape, scale):
        return jax.random.normal(k, shape, jnp.float32) * scale

    return {
        "x": nrm(ks[0], (BATCH, SEQ, D), 1.0),
        "norm1_g": 1.0 + nrm(ks[1], (L, D), 0.02),
        "w_in": nrm(ks[2], (L, D, IN_COLS), D ** -0.5),
        "conv_mix_w": nrm(ks[3], (L, CONV_TAPS, CONV_W), 0.5),
        "w_alpha_up": nrm(ks[4], (L, ALPHA_RANK, KEY_DIM), ALPHA_RANK ** -0.5),
        "b_alpha": 0.5 + nrm(ks[5], (L, KEY_DIM), 0.1),
        "gla_norm_g": 1.0 + nrm(ks[6], (L, VALUE_DIM), 0.02),
        "w_conv_out": nrm(ks[7], (L, CONV_W, D), CONV_W ** -0.5),
        "w_gla_out": nrm(ks[8], (L, VALUE_DIM, D), VALUE_DIM ** -0.5),
        "b_gate": nrm(ks[9], (L, 2 * D), 0.02),
        "w_o": nrm(ks[10], (L, D, D), D ** -0.5),
        "norm2_g": 1.0 + nrm(ks[11], (L, D), 0.02),
        "w_up": nrm(ks[12], (L, D, 2 * D_FF), D ** -0.5),
        "ffn_conv_w": nrm(ks[13], (L, FFN_TAPS, 2 * D_FF), 0.5),
        "w_down": nrm(ks[14], (L, D_FF, D), D_FF ** -0.5),
        "final_g": 1.0 + nrm(ks[15], (D,), 0.02),
    }


def reference(x, norm1_g, w_in, conv_mix_w, w_alpha_up, b_alpha, gla_norm_g,
              w_conv_out, w_gla_out, b_gate, w_o, norm2_g, w_up, ffn_conv_w,
              w_down, final_g):
    bsz, s, _ = x.shape
    for l in range(DEPTH):
        h = rmsnorm(x, norm1_g[l])
        proj = h @ w_in[l]
        c_h, c_b, c_c, q, k, v, g_out, a_low, g_a, g_b = jnp.split(proj, IN_SPLITS, axis=-1)

        branch_a = (c_b * causal_dwconv3(c_c * c_h, conv_mix_w[l])) @ w_conv_out[l]

        log_a = jax.nn.log_sigmoid(a_low.astype(jnp.float32) @ w_alpha_up[l].astype(jnp.float32)
                                   + b_alpha[l].astype(jnp.float32)) / GATE_TAU
        qh = q.reshape(bsz, s, GLA_HEADS, HEAD_K) * (HEAD_K ** -0.5)
        kh = k.reshape(bsz, s, GLA_HEADS, HEAD_K)
        vh = v.reshape(bsz, s, GLA_HEADS, HEAD_V)
        la = log_a.reshape(bsz, s, GLA_HEADS, HEAD_K)
        o = gla_chunked(qh, kh, vh, la)
        o = rmsnorm(o, gla_norm_g[l].reshape(GLA_HEADS, HEAD_V)).reshape(bsz, s, VALUE_DIM)
        branch_b = (o * jax.nn.silu(g_out)) @ w_gla_out[l]

        gate_a, gate_b = jnp.split(jax.nn.sigmoid(jnp.concatenate([g_a, g_b], axis=-1) + b_gate[l]), 2, axis=-1)
        x = x + (gate_a * branch_a + gate_b * branch_b) @ w_o[l]

        h2 = rmsnorm(x, norm2_g[l])
        u = causal_dwconv3(h2 @ w_up[l], ffn_conv_w[l])
        u_gate, u_val = jnp.split(u, 2, axis=-1)
        x = x + (jax.nn.silu(u_gate) * u_val) @ w_down[l]
    return rmsnorm(x, final_g)
```

```python
import numpy as np
from contextlib import ExitStack
import concourse.bass as bass
import concourse.mybir as mybir
from concourse.bass_utils import run_bass_kernel_spmd

F32 = mybir.dt.float32
BF16 = mybir.dt.bfloat16
AF = mybir.ActivationFunctionType
ALU = mybir.AluOpType
EPS = 1e-6
NB = 256


class Cfg:
    def __init__(self, D=4096, CW=2048, H=8, DFF=11008, L=2, BATCH=2, SEQ=4096, NCORE=8, KG=16, use_cc=None):
        self.D, self.CW, self.H, self.DFF, self.L = D, CW, H, DFF, L
        self.KG = KG
        self.use_cc = (NCORE > BATCH) if use_cc is None else use_cc
        self.BATCH, self.SEQ, self.NCORE = BATCH, SEQ, NCORE
        self.DK, self.DV, self.R = 128, 256, 16
        self.KEYD, self.VALD = H * 128, H * 256
        self.KD, self.CWC, self.FC = D // 128, CW // 128, DFF // 128
        self.NG = NCORE // BATCH
        self.TC = SEQ // self.NG
        self.T = 512
        self.NT = self.TC // self.T
        sizes = (CW, CW, CW, self.KEYD, self.KEYD, self.VALD, self.VALD, self.R, D, D)
        offs = np.concatenate([[0], np.cumsum(sizes)]).astype(int)
        self.IN_COLS = int(offs[-1])
        names = ("ch", "cb", "cc", "q", "k", "v", "go", "al", "ga", "gb")
        self.off = {n: int(o) for n, o in zip(names, offs[:-1])}
        self.groups = self._groups()

    def _groups(self):
        c = self
        g = []
        win = []
        win += [("k", i) for i in range(c.KEYD // NB)]
        win += [("v", i) for i in range(c.VALD // NB)]
        win += [("cc", i) for i in range(c.CW // NB)]
        win += [("ch", i) for i in range(c.CW // NB)]
        win += [("cb", i) for i in range(c.CW // NB)]
        win += [("q", i) for i in range(c.KEYD // NB)]
        win += [("go", i) for i in range(c.VALD // NB)]
        win += [("ga", i) for i in range(c.D // NB)]
        win += [("gb", i) for i in range(c.D // NB)]
        per = 2 * c.NCORE
        for j in range(0, len(win), per):
            g.append((f"win{j // per}", c.KD, win[j:j + per]))
        g.append(("co", c.CWC, [("co", i) for i in range(c.D // NB)]))
        g.append(("gl", c.VALD // 128, [("gl", i) for i in range(c.D // NB)]))
        g.append(("wo", c.KD, [("wo", i) for i in range(c.D // NB)]))
        up = [("up", f) for f in range(c.FC)]
        per = 4 * c.NCORE
        for j in range(0, len(up), per):
            g.append((f"up{j // per}", c.KD, up[j:j + per]))
        g.append(("dn", c.FC, [("dn", i) for i in range(c.D // NB)]))
        return g


FULL = Cfg(NCORE=2)


def _block_cols(cfg, inputs, l, name, i):
    c = cfg
    if name in ("k", "v", "cc", "ch", "cb", "q", "go", "ga", "gb"):
        o = c.off[name] + i * NB
        return inputs["w_in"][l][:, o:o + NB]
    if name == "co":
        return inputs["w_conv_out"][l][:, i * NB:(i + 1) * NB]
    if name == "gl":
        return inputs["w_gla_out"][l][:, i * NB:(i + 1) * NB]
    if name == "wo":
        return inputs["w_o"][l][:, i * NB:(i + 1) * NB]
    if name == "up":
        w = inputs["w_up"][l]
        return np.concatenate([w[:, i * 128:(i + 1) * 128],
                               w[:, c.DFF + i * 128:c.DFF + (i + 1) * 128]], axis=1)
    if name == "dn":
        return inputs["w_down"][l][:, i * NB:(i + 1) * NB]
    raise KeyError(name)


def _pmajor(v, nch):
    return np.ascontiguousarray(np.asarray(v, np.float32).reshape(nch, 128).T)


def make_in_maps(cfg, inputs):
    c = cfg
    inputs = {k: np.asarray(v, np.float32) for k, v in inputs.items()}
    maps = [dict() for _ in range(c.NCORE)]
    for l in range(c.L):
        for (pname, KC, blocks) in c.groups:
            bpr = -(-len(blocks) // c.NCORE) if c.use_cc else len(blocks)
            shared = None
            for r in range(c.NCORE):
                if not c.use_cc and shared is not None:
                    maps[r][f"w_{l}_{pname}"] = shared
                    continue
                arr = np.zeros((bpr, 128, KC, NB), np.float32)
                for i in range(bpr):
                    b = (r * bpr + i) if c.use_cc else i
                    if b < len(blocks):
                        blk = _block_cols(c, inputs, l, *blocks[b])
                        arr[i] = blk.reshape(KC, 128, NB).transpose(1, 0, 2)
                shared = arr.reshape(bpr * 128 * KC, NB)
                maps[r][f"w_{l}_{pname}"] = shared
    x = inputs["x"]
    for r in range(c.NCORE):
        b, p = r // c.NG, r % c.NG
        maps[r]["xT"] = np.ascontiguousarray(x[b, p * c.TC:(p + 1) * c.TC, :].T)
    rows = []
    for l in range(c.L):
        cols = []
        cols.append(_pmajor(inputs["norm1_g"][l], c.KD))
        cols.append(_pmajor(inputs["norm2_g"][l], c.KD))
        cols.append(np.stack([_pmajor(inputs["conv_mix_w"][l][t], c.CWC) for t in range(3)], -1)
                    .reshape(128, c.CWC * 3))
        cols.append(np.stack([_pmajor(inputs["ffn_conv_w"][l][t], 2 * c.FC) for t in range(3)], -1)
                    .reshape(128, 2 * c.FC * 3))
        cols.append(_pmajor(inputs["b_gate"][l], 2 * c.KD))
        cols.append(_pmajor(inputs["gla_norm_g"][l], c.H * 2))
        o = c.off["al"]
        cols.append(inputs["w_in"][l][:, o:o + c.R].reshape(c.KD, 128, c.R).transpose(1, 0, 2)
                    .reshape(128, c.KD * c.R))
        row = np.concatenate(cols, axis=1)
        npl = -(-row.shape[1] // 128) * 128
        rows.append(np.concatenate([row, np.zeros((128, npl - row.shape[1]), np.float32)], axis=1))
    params = np.ascontiguousarray(np.concatenate(rows, axis=0))
    gf = _pmajor(inputs["final_g"], c.KD)
    wa = np.zeros((c.L * 32, c.KEYD), np.float32)
    for l in range(c.L):
        wa[l * 32:l * 32 + c.R, :] = inputs["w_alpha_up"][l]
        wa[l * 32 + c.R, :] = inputs["b_alpha"][l]
    j = np.arange(128)[:, None]
    i = np.arange(128)[None, :]
    cf = np.zeros((128, 5 * 128), np.float32)
    cf[:, 0:128] = np.where(j <= i, -1.0 / 16, 0.0)
    cf[:, 128:256] = np.where(j > i, -1.0 / 16, 0.0)
    cf[:, 256:384] = np.where(j <= i, 1.0, 0.0)
    cf[:, 384:512] = np.eye(128)
    cf[:, 512:640] = 1.0
    for r in range(c.NCORE):
        p = r % c.NG
        m = np.zeros((128, 16), np.float32)
        for q in range(c.NG):
            m[:, q] = 1.0 if q < p else 0.0
            m[:, 8 + q] = 1.0 if q == p - 1 else 0.0
        maps[r]["prm"] = params
        maps[r]["wa"] = wa
        maps[r]["gf"] = gf
        maps[r]["cf"] = cf
        maps[r]["cm"] = m
    return maps


class _Res:
    __slots__ = ("w", "r")

    def __init__(self):
        self.w = None
        self.r = {}


class Sched:
    def __init__(self, nc, sems, engines):
        self.nc = nc
        self.sems = sems
        self.eng = engines
        self.reset(None)

    def reset(self, active):
        self.active = active
        self.dry = active is None
        self.res = {}
        self.cnt = {k: 0 for k in self.sems}
        self.waited = {e: {} for e in self.eng}
        self.nins = {e: 0 for e in self.eng}

    def _R(self, k):
        r = self.res.get(k)
        if r is None:
            r = self.res[k] = _Res()
        return r

    def _wait(self, e, reads, writes):
        toks = {}
        for k in reads:
            r = self._R(k)
            if r.w is not None and toks.get(r.w[0], 0) < r.w[1]:
                toks[r.w[0]] = r.w[1]
            if type(k) is tuple and k[0] == "ps":
                for s, v in r.r.items():
                    if s != e and toks.get(s, 0) < v:
                        toks[s] = v
        for k in writes:
            r = self._R(k)
            if r.w is not None and toks.get(r.w[0], 0) < r.w[1]:
                toks[r.w[0]] = r.w[1]
            for s, v in r.r.items():
                if toks.get(s, 0) < v:
                    toks[s] = v
        wd = self.waited[e]
        for s, v in toks.items():
            if e == "pe" and s == "pe":
                continue
            if wd.get(s, 0) < v:
                if self.active == e:
                    self.eng[e].wait_ge(self.sems[s], v)
                    self.nins[e] += 1
                wd[s] = v

    def _commit(self, reads, writes, tok):
        s, v = tok
        for k in reads:
            r = self._R(k)
            if r.r.get(s, 0) < v:
                r.r[s] = v
        for k in writes:
            r = self._R(k)
            r.w = tok
            r.r = {}

    def op(self, e, fn, reads=(), writes=(), signal=True):
        self._wait(e, reads, writes)
        self.nins[e] += 1
        if signal:
            self.cnt[e] += 1
            tok = (e, self.cnt[e])
            if self.active == e:
                fn().then_inc(self.sems[e], 1)
        else:
            tok = (e, self.cnt[e] + 1)
            if self.active == e:
                fn()
        self._commit(reads, writes, tok)

    def dma(self, q, sem, fn, reads=(), writes=(), inc=16, serial=False):
        self._wait(q, reads, writes)
        self.nins[q] += 1
        self.cnt[sem] += inc
        tok = (sem, self.cnt[sem])
        if self.active == q:
            if inc == 1:
                fn().then_inc(self.sems[sem])
            else:
                fn().then_inc(self.sems[sem], inc)
        if serial:
            if self.active == q:
                self.eng[q].wait_ge(self.sems[sem], self.cnt[sem])
            self.waited[q][sem] = self.cnt[sem]
        self._commit(reads, writes, tok)


class Ring:
    def __init__(self, S, slots, issue_fn):
        self.S, self.n, self.issue_fn = S, slots, issue_fn
        self.req = []
        self.i = 0
        self.issued = 0

    def restart(self):
        self.i = 0
        self.issued = 0

    def get_many(self, specs):
        assert len(specs) <= self.n
        if self.S.dry:
            return [self.get(s) for s in specs]
        first = self.get(specs[0])
        out = [first]
        for _ in specs[1:]:
            out.append(self.i % self.n)
            self.i += 1
        assert self.issued >= self.i or self.issued == len(self.req), (self.issued, self.i)
        return out

    def get(self, spec):
        if self.S.dry:
            self.req.append(spec)
            slot = self.i % self.n
            self.i += 1
            return slot
        hi = min(len(self.req), self.i + self.n)
        while self.issued < hi:
            j = self.issued
            self.issue_fn(j % self.n, self.req[j])
            self.issued += 1
        slot = self.i % self.n
        self.i += 1
        return slot


def build_program(cfg):
    c = cfg
    T, KD, CWC, FC, H, L, NG, KG = c.T, c.KD, c.CWC, c.FC, c.H, c.L, c.NG, c.KG
    VC = c.VALD // 128
    NS = T // 128
    R1 = c.R + 1
    nc = bass.Bass("TRN2", target_bir_lowering=False)
    es = ExitStack()

    xT_in = nc.dram_tensor("xT", [c.D, c.TC], F32, kind="ExternalInput")
    outT = nc.dram_tensor("outT", [c.D, c.TC], F32, kind="ExternalOutput")
    wsrc, wstage, wgath, wbpr = {}, {}, {}, {}
    for l in range(L):
        for (pname, KC, blocks) in c.groups:
            bpr = -(-len(blocks) // c.NCORE) if c.use_cc else len(blocks)
            rows = bpr * 128 * KC
            key = (l, pname)
            wbpr[key] = bpr
            wsrc[key] = nc.dram_tensor(f"w_{l}_{pname}", [rows, NB], F32, kind="ExternalInput")
            wstage[key] = nc.dram_tensor(f"ws_{l}_{pname}", [rows, NB], BF16)
            wgath[key] = nc.dram_tensor(f"wg_{l}_{pname}", [c.NCORE * rows, NB], BF16) if c.use_cc else wstage[key]
    poff = {}
    o = 0
    for nm, n in (("g1", KD), ("g2", KD), ("cw", CWC * 3), ("fw", 2 * FC * 3),
                  ("bg", 2 * KD), ("gn", H * 2), ("al", KD * c.R)):
        poff[nm] = o
        o += n
    NPL = -(-o // 128) * 128
    params_d = nc.dram_tensor("prm", [L * 128, NPL], F32, kind="ExternalInput")
    gf_d = nc.dram_tensor("gf", [128, KD], F32, kind="ExternalInput")
    wa_d = nc.dram_tensor("wa", [L * 32, c.KEYD], F32, kind="ExternalInput")
    cf_d = nc.dram_tensor("cf", [128, 640], F32, kind="ExternalInput")
    cm_d = nc.dram_tensor("cm", [128, 16], F32, kind="ExternalInput")
    xmid = nc.dram_tensor("xmid", [c.D, c.TC], F32)
    xlay = [nc.dram_tensor(f"xlay{l}", [c.D, c.TC], F32) for l in range(L)]
    GW = H * 256 + H
    xg_src = nc.dram_tensor("xg_src", [128, GW], F32)
    xg_dst = nc.dram_tensor("xg_dst", [NG * 128, GW], F32)
    xh_src = nc.dram_tensor("xh_src", [128, KD * 2], F32)
    xh_dst = nc.dram_tensor("xh_dst", [NG * 128, KD * 2], F32)
    groups_cc = [[b * NG + q for q in range(NG)] for b in range(c.BATCH)]

    def sb(name, shape, dt):
        return es.enter_context(nc.sbuf_tensor(name, shape, dt))

    params = sb("params_s", [128, NPL], F32)
    gf = sb("gf_s", [128, KD], F32)
    wa = sb("wa_s", [32, c.KEYD], F32)
    cf = sb("cf_s", [128, 384], F32)
    cm = sb("cm_s", [128, 16], F32)
    cb = sb("cb_s", [128, 128], BF16)
    cbf = sb("cbf_s", [128, 128], F32)
    onesD = sb("onesD", [128, 128], BF16)
    onesV = sb("onesV", [128, 128], BF16)
    walow = sb("walow", [128, KD * c.R], BF16)
    NW = 4
    wslot = [sb(f"wslot{i}", [128, KG, NB], BF16) for i in range(NW)]
    NX = 3
    xslot = [sb(f"xslot{i}", [128, T], F32) for i in range(NX)]
    NO = 2
    oslot = [sb(f"oslot{i}", [128, T], F32) for i in range(NO)]
    hbuf = sb("hbuf", [128, KD, T], BF16)
    hh = sb("hh", [128, KD, 2], BF16)
    xhal = sb("xhal", [128, KD * 2], F32)
    xhg = sb("xhg", [128, NG, KD * 2], F32)
    hstage = sb("hstage", [128, KD, 2], F32)
    sqb = [sb(f"sqb{i}", [128, T], BF16) for i in range(2)]
    sqh = sb("sqh", [128, KD * 2], BF16)
    rstd = sb("rstd", [128, T], F32)
    rstdh = sb("rstdh", [128, 2], F32)
    atmp = sb("atmp", [128, H], F32)
    Dtot = sb("Dtot", [128, H], F32)
    alT = sb("alT", [32, T], F32)
    alH = sb("alH", [32, T], BF16)
    alL = sb("alL", [32, T], BF16)
    waH = sb("waH", [32, c.KEYD], BF16)
    waL = sb("waL", [32, c.KEYD], BF16)
    cfb = sb("cfb", [128, 256], BF16)
    oneb = sb("oneb", [128, 1], F32)
    PT = [sb(f"PT{i}", [128, 128], BF16) for i in range(2)]
    accb = [sb(f"accb{i}", [128, T + 2], F32) for i in range(4)]
    tmpb = [sb(f"tmpb{i}", [128, T], F32) for i in range(2)]
    uh_save = sb("uh_save", [128, CWC, 2], F32)
    fh_save = sb("fh_save", [128, 2 * FC, 2], F32)
    hsm = sb("hsm", [128, 16], F32)
    epsb = sb("epsb", [128, 1], F32)

    def nbytes(shape, dt):
        return int(np.prod(shape)) * (2 if dt == BF16 else 4)

    lay_mix = [("y_a", [CWC, T], BF16), ("y_b", [VC, T], BF16), ("Sst", [H, 256], F32), ("Sbf", [H, 256], BF16)]
    lay_gla = [("q_inT", [2, T], BF16), ("k_inT", [2, T], BF16), ("kT_raw", [2, T], BF16),
               ("k_out", [NS, 256], BF16), ("v_tok", [NS, 512], BF16), ("cbuf", [NS, 256], F32), ("cH", [NS, 256], BF16), ("cL", [NS, 256], BF16),
               ("Eout", [NS, 256], F32), ("Ein", [2, T], F32), ("Einv", [2, T], F32), ("sgb", [4, T], BF16)]
    base = sum(nbytes(s_, d_) for _, s_, d_ in lay_mix)
    gsz = sum(nbytes(s_, d_) for _, s_, d_ in lay_gla)
    BIGB = max(base + max(gsz, nbytes([KD, T], BF16), GW * 4), nbytes([FC, T], BF16))
    big = sb("big", [128, BIGB], mybir.dt.uint8)

    def carve(o0, shape, dt):
        ap = big[:, o0:o0 + nbytes(shape, dt)].bitcast(dt)
        return ap.rearrange("p (a b) -> p a b", a=shape[0], b=shape[1])

    V = {}
    o = 0
    for nm, s_, d_ in lay_mix:
        V[nm] = carve(o, s_, d_)
        o += nbytes(s_, d_)
    for nm, s_, d_ in lay_gla:
        V[nm] = carve(o, s_, d_)
        o += nbytes(s_, d_)
    y_a, y_b, Sst, Sbf = V["y_a"], V["y_b"], V["Sst"], V["Sbf"]
    q_inT, k_inT, kT_raw, k_out, v_tok = V["q_inT"], V["k_inT"], V["kT_raw"], V["k_out"], V["v_tok"]
    cbuf, Eout, Ein, Einv, sgb = V["cbuf"], V["Eout"], V["Ein"], V["Einv"], V["sgb"]
    cH, cL = V["cH"], V["cL"]
    zbuf = carve(base, [KD, T], BF16)
    gtmp = big[:, base:base + GW * 4].bitcast(F32)
    abuf = carve(0, [FC, T], BF16)
    GLA_KEYS = [nm for nm, _, _ in lay_gla]
    S_KEYS = [("S", hi) for hi in range(H)] + [("Sbf", hi) for hi in range(H)]

    ps = [es.enter_context(nc.psum_tensor(f"ps{i}", [128, 512], F32)) for i in range(8)]

    semnames = ["pe", "act", "dve", "pool", "gq", "cc", "ld"] + \
               [f"w{i}" for i in range(NW)] + [f"x{i}" for i in range(NX)] + \
               [f"o{i}" for i in range(NO)]
    sems = {n: es.enter_context(nc.semaphore(n)) for n in semnames}
    for _s in sems.values():
        nc.gpsimd.sem_clear(_s)
    nc.all_engine_barrier()
    block = es.enter_context(nc.Block())
    engines = {"pe": nc.tensor, "act": nc.scalar, "dve": nc.vector, "pool": nc.gpsimd, "sp": nc.sync}
    S = Sched(nc, sems, engines)

    def P(nm, a=0, n=1):
        return params[:, poff[nm] + a:poff[nm] + a + n]

    def issue_w(slot, spec):
        key, blk, k0, kcnt, KC = spec
        S.dma("sp", f"w{slot}",
              lambda: nc.sync.dma_start(
                  out=wslot[slot][:, 0:kcnt, :],
                  in_=wgath[key].ap()[blk * 128 * KC:(blk + 1) * 128 * KC, :]
                  .rearrange("(p k) c -> p k c", p=128, k=KC)[:, k0:k0 + kcnt, :]),
              reads=[("wg", key)], writes=[("wslot", slot)])

    WR = Ring(S, NW, issue_w)

    def issue_x(slot, spec):
        src_t, kc, t = spec
        S.dma("sp", f"x{slot}",
              lambda: nc.sync.dma_start(out=xslot[slot][:, :],
                                        in_=src_t.ap()[kc * 128:(kc + 1) * 128, t * T:(t + 1) * T]),
              reads=[(src_t.name, kc, t)], writes=[("xslot", slot)])

    XR = Ring(S, NX, issue_x)
    ocount = [0]

    def store_chunk(dst_t, kc, t, compute):
        slot = ocount[0] % NO
        ocount[0] += 1
        compute(oslot[slot], ("oslot", slot))
        S.dma("act", f"o{slot}",
              lambda: nc.scalar.dma_start(out=dst_t.ap()[kc * 128:(kc + 1) * 128, t * T:(t + 1) * T],
                                          in_=oslot[slot][:, :]),
              reads=[("oslot", slot)], writes=[(dst_t.name, kc, t)])

    blkpos = {}
    for (pname, KC, blocks) in c.groups:
        for b, (nm, i) in enumerate(blocks):
            blkpos[(nm, i)] = (pname, b, KC)

    def kgroups(KC):
        return [(k0, min(KG, KC - k0)) for k0 in range(0, KC, KG)]

    def gemm(l, nm, i, rhs_fn, rkeys, banks, n=T, extra=None):
        pname, b, KC = blkpos[(nm, i)]
        grp = kgroups(KC)
        for gi, (k0, kcnt) in enumerate(grp):
            slot = WR.get(((l, pname), b, k0, kcnt, KC))
            first, last = gi == 0, gi == len(grp) - 1
            for m in range(2):
                for k in range(kcnt):
                    S.op("pe", lambda: nc.tensor.matmul(
                        ps[banks[m]][:, 0:n], lhsT=wslot[slot][:, k, m * 128:(m + 1) * 128],
                        rhs=rhs_fn(k0 + k), start=(first and k == 0), stop=(last and k == kcnt - 1)),
                        reads=[("wslot", slot)] + rkeys, writes=[("ps", banks[m])],
                        signal=(k == kcnt - 1))
            if extra is not None:
                extra(slot, k0, kcnt, first, last)

    def halo_mm(col0, hkey="hh"):
        def f(slot, k0, kcnt, first, last):
            for m in range(2):
                for k in range(kcnt):
                    S.op("pe", lambda: nc.tensor.matmul(
                        ps[6 + m][:, col0:col0 + 2], lhsT=wslot[slot][:, k, m * 128:(m + 1) * 128],
                        rhs=hh[:, k0 + k, :], start=(first and k == 0), stop=(last and k == kcnt - 1)),
                        reads=[("wslot", slot), hkey], writes=[("ps", 6 + m)], signal=(k == kcnt - 1))
        return f

    def fence(reads, writes):
        S.op("dve", lambda: nc.vector.memset(hsm[:, 15:16], 0.0), reads=reads, writes=writes + ["hsm15"])

    def program():
        ocount[0] = 0
        hfun = lambda k: hbuf[:, k, :]
        S.dma("sp", "ld", serial=True, fn=lambda: nc.sync.dma_start(out=cf[:, :], in_=cf_d.ap()[:, 0:384]), writes=["cf"])
        S.dma("sp", "ld", serial=True, fn=lambda: nc.sync.dma_start(out=cbf[:, :], in_=cf_d.ap()[:, 384:512]), writes=["cbf"])
        S.dma("sp", "ld", serial=True, fn=lambda: nc.sync.dma_start(out=cm[:, :], in_=cm_d.ap()[:, :]), writes=["cm"])
        S.dma("sp", "ld", serial=True, fn=lambda: nc.sync.dma_start(out=gf[:, :], in_=gf_d.ap()[:, :]), writes=["gf"])
        S.op("dve", lambda: nc.vector.tensor_copy(cb[:, :], cbf[:, :]), reads=["cbf"], writes=["cb"])
        S.op("dve", lambda: nc.vector.memset(onesD[:, :], 1.0 / c.D), writes=["onesD"])
        S.op("dve", lambda: nc.vector.memset(onesV[:, :], 1.0 / c.DV), writes=["onesV"])
        S.op("dve", lambda: nc.vector.memset(alT[:, :], 1.0), writes=["alT"])
        S.op("dve", lambda: nc.vector.memset(oneb[:, :], 1.0), writes=["oneb"])
        S.op("dve", lambda: nc.vector.tensor_copy(cfb[:, :], cf[:, 0:256]), reads=["cf"], writes=["cfb"])
        S.op("dve", lambda: nc.vector.memset(epsb[:, :], EPS), writes=["epsb"])
        import os, time as _time
        S.op("dve", lambda: nc.vector.memset(hsm[:, 14:15], float(NONCE)), writes=["hsm14"])
        ident = cb[:, 0:128]
        TriU, M2, mask01 = cfb[:, 0:128], cfb[:, 128:256], cf[:, 256:384]

        def load_params(l):
            import os
            LP = int(os.environ.get("K_LP", "7"))
            if LP & 1:
                S.dma("sp", "ld", serial=True, fn=lambda: nc.sync.dma_start(out=params[:, :], in_=params_d.ap()[l * 128:(l + 1) * 128, :]),
                  writes=["params"])
            if LP & 2:
                S.dma("sp", "ld", serial=True, fn=lambda: nc.sync.dma_start(out=wa[:, :], in_=wa_d.ap()[l * 32:(l + 1) * 32, :]),
                  writes=["wa"])
            if LP & 4:
                S.op("dve", lambda: nc.vector.tensor_copy(walow[:, :], P("al", 0, KD * c.R)),
                 reads=["params"], writes=["walow"])
                S.op("dve", lambda: nc.vector.tensor_copy(waH[:, :], wa[:, :]), reads=["wa"], writes=["waH"])
                S.op("dve", lambda: nc.vector.tensor_tensor(waL[:, :], wa[:, :], waH[:, :], ALU.subtract),
                     reads=["wa", "waH"], writes=["waL"])

        def gather_weights(l):
            import os
            GV = int(os.environ.get("K_GV", "3"))
            for (pname, KC, blocks) in c.groups:
                key = (l, pname)
                rows = wbpr[key] * 128 * KC
                step = 4096
                for r0 in range(0, rows if GV & 1 else 0, step):
                    r1 = min(rows, r0 + step)
                    S.dma("pool", "gq", serial=True, fn=lambda: nc.gpsimd.dma_start(out=wstage[key].ap()[r0:r1, :], in_=wsrc[key].ap()[r0:r1, :]),
                          reads=[], writes=[("ws", key)] if c.use_cc else [("wg", key)])
                if (GV & 2) and c.use_cc:
                  S.dma("pool", "cc",
                      lambda: nc.gpsimd.collective_compute(
                          "AllGather", ALU.bypass, replica_groups=[list(range(c.NCORE))],
                          ins=[wstage[key].ap().opt()], outs=[wgath[key].ap().opt()]),
                      reads=[("ws", key)], writes=[("wg", key)], inc=1, serial=True)

        def exchange(src_t, dst_t, src_sb_ap, rkeys, dkey):
            if not c.use_cc:
                return
            S.dma("pool", "gq", serial=True, fn=lambda: nc.gpsimd.dma_start(out=src_t.ap()[:, :], in_=src_sb_ap),
                  reads=rkeys, writes=[("xsrc", src_t.name)])
            S.dma("pool", "cc",
                  lambda: nc.gpsimd.collective_compute(
                      "AllGather", ALU.bypass, replica_groups=groups_cc,
                      ins=[src_t.ap().opt()], outs=[dst_t.ap().opt()]),
                  reads=[("xsrc", src_t.name)], writes=[dkey], inc=1, serial=True)

        def sumsq(src_t, t, per_chunk=None):
            for kc in range(KD):
                xs = XR.get((src_t, kc, t))
                S.op("act", lambda: nc.scalar.activation(sqb[kc % 2][:, :], xslot[xs][:, :], AF.Square),
                     reads=[("xslot", xs)], writes=[("sqb", kc % 2)])
                S.op("pe", lambda: nc.tensor.matmul(ps[7][:, 0:T], lhsT=onesD[:, :], rhs=sqb[kc % 2][:, :],
                                                    start=(kc == 0), stop=(kc == KD - 1)),
                     reads=[("sqb", kc % 2), "onesD"], writes=[("ps", 7)])
                if per_chunk is not None:
                    per_chunk(kc, xs)
            S.op("act", lambda: nc.scalar.activation(rstd[:, :], ps[7][:, 0:T], AF.Sqrt, bias=epsb[:, 0:1]),
                 reads=[("ps", 7), "epsb"], writes=["rstd"])
            S.op("dve", lambda: nc.vector.reciprocal(rstd[:, :], rstd[:, :]),
                 reads=["rstd"], writes=["rstd"])

        def norm_tile(src_t, t, gname, capture=False):
            def pc(kc, xs):
                S.op("dve", lambda: nc.vector.tensor_scalar(hbuf[:, kc, :], xslot[xs][:, :], P(gname, kc, 1), None, ALU.mult),
                     reads=[("xslot", xs), "params"], writes=["h"])
                if capture:
                    S.op("dve", lambda: nc.vector.tensor_copy(hstage[:, kc, :], xslot[xs][:, T - 2:T]),
                         reads=[("xslot", xs)], writes=["hstage"])
            sumsq(src_t, t, pc)
            for kc in range(KD):
                S.op("dve", lambda: nc.vector.tensor_tensor(hbuf[:, kc, :], hbuf[:, kc, :], rstd[:, :], ALU.mult),
                     reads=["rstd", "h"], writes=["h"])

        def halo_tile(gname):
            if not c.use_cc:
                S.op("dve", lambda: nc.vector.memset(hh[:, :, :], 0.0), writes=["hh"])
                return
            S.dma("pool", "gq", serial=True, fn=lambda: nc.gpsimd.dma_start(
                out=xhg[:, :, :], in_=xh_dst.ap().rearrange("(r p) f -> p r f", p=128)),
                reads=["xh_dst"], writes=["xhg"])
            for q in range(NG):
                if q == 0:
                    S.op("dve", lambda: nc.vector.tensor_scalar(xhal[:, :], xhg[:, 0, :], cm[:, 8:9], None, ALU.mult),
                         reads=["xhg", "cm"], writes=["xhal"])
                else:
                    S.op("dve", lambda: nc.vector.scalar_tensor_tensor(
                        xhal[:, :], xhg[:, q, :], cm[:, 8 + q:9 + q], xhal[:, :], ALU.mult, ALU.add),
                        reads=["xhg", "cm", "xhal"], writes=["xhal"])
            S.op("act", lambda: nc.scalar.activation(sqh[:, :], xhal[:, :], AF.Square), reads=["xhal"], writes=["sqh"])
            for kc in range(KD):
                S.op("pe", lambda: nc.tensor.matmul(ps[7][:, 0:2], lhsT=onesD[:, :], rhs=sqh[:, kc * 2:kc * 2 + 2],
                                                    start=(kc == 0), stop=(kc == KD - 1)),
                     reads=["sqh", "onesD"], writes=[("ps", 7)], signal=(kc == KD - 1))
            S.op("act", lambda: nc.scalar.activation(rstdh[:, :], ps[7][:, 0:2], AF.Sqrt, bias=epsb[:, 0:1]),
                 reads=[("ps", 7), "epsb"], writes=["rstdh"])
            S.op("dve", lambda: nc.vector.reciprocal(rstdh[:, :], rstdh[:, :]),
                 reads=["rstdh"], writes=["rstdh"])
            for kc in range(KD):
                S.op("dve", lambda: nc.vector.scalar_tensor_tensor(
                    hh[:, kc, :], xhal[:, kc * 2:kc * 2 + 2], P(gname, kc, 1), rstdh[:, :], ALU.mult, ALU.mult),
                    reads=["xhal", "rstdh", "params"], writes=["hh"])

        def conv_from(acc, akey, src, skeys, halo, hkeys, w0, w1, w2):
            S.op("act", lambda: nc.scalar.activation(acc[:, 0:T], src, AF.Copy, scale=w2),
                 reads=skeys + ["params"], writes=[akey])
            S.op("dve", lambda: nc.vector.scalar_tensor_tensor(acc[:, 1:T], src[:, 0:T - 1], w1, acc[:, 1:T], ALU.mult, ALU.add),
                 reads=skeys + [akey, "params"], writes=[akey])
            S.op("dve", lambda: nc.vector.scalar_tensor_tensor(acc[:, 2:T], src[:, 0:T - 2], w0, acc[:, 2:T], ALU.mult, ALU.add),
                 reads=skeys + [akey, "params"], writes=[akey])
            S.op("dve", lambda: nc.vector.scalar_tensor_tensor(acc[:, 0:2], halo, w0, acc[:, 0:2], ALU.mult, ALU.add),
                 reads=hkeys + [akey, "params"], writes=[akey])
            S.op("dve", lambda: nc.vector.scalar_tensor_tensor(acc[:, 0:1], halo[:, 1:2], w1, acc[:, 0:1], ALU.mult, ALU.add),
                 reads=hkeys + [akey, "params"], writes=[akey])

        def mixer_a(l, t):
            for i in range(CWC // 2):
                gemm(l, "cc", i, hfun, ["h"], (0, 1), extra=halo_mm(0) if t == 0 else None)
                gemm(l, "ch", i, hfun, ["h"], (2, 3), extra=halo_mm(4) if t == 0 else None)
                if t == 0:
                    for m in range(2):
                        S.op("act", lambda: nc.scalar.copy(hsm[:, m * 2:m * 2 + 2], ps[6 + m][:, 0:2]),
                             reads=[("ps", 6 + m)], writes=["hsm"])
                for m in range(2):
                    ch = 2 * i + m
                    U = accb[2 + m]
                    uk = ("acc", 2 + m)
                    S.op("act", lambda: nc.scalar.copy(tmpb[m][:, :], ps[m][:, 0:T]),
                         reads=[("ps", m)], writes=[("tmpb", m)])
                    S.op("dve", lambda: nc.vector.tensor_tensor(U[:, 2:T + 2], tmpb[m][:, :], ps[2 + m][:, 0:T], ALU.mult),
                         reads=[("tmpb", m), ("ps", 2 + m)], writes=[uk])
                    if t == 0:
                        S.op("dve", lambda: nc.vector.tensor_tensor(
                            U[:, 0:2], hsm[:, m * 2:m * 2 + 2], ps[6 + m][:, 4:6], ALU.mult),
                            reads=["hsm", ("ps", 6 + m)], writes=[uk])
                    else:
                        S.op("dve", lambda: nc.vector.tensor_copy(U[:, 0:2], uh_save[:, ch, :]),
                             reads=["uh_save"], writes=[uk])
                    S.op("dve", lambda: nc.vector.tensor_copy(uh_save[:, ch, :], U[:, T:T + 2]),
                         reads=[uk], writes=["uh_save"])
                    conv_from(accb[m], ("acc", m), U[:, 2:T + 2], [uk], U[:, 0:2], [uk],
                              P("cw", ch * 3 + 0, 1), P("cw", ch * 3 + 1, 1), P("cw", ch * 3 + 2, 1))
                gemm(l, "cb", i, hfun, ["h"], (4, 5))
                for m in range(2):
                    ch = 2 * i + m
                    S.op("dve", lambda: nc.vector.tensor_tensor(y_a[:, ch, :], accb[m][:, 0:T], ps[4 + m][:, 0:T], ALU.mult),
                         reads=[("acc", m), ("ps", 4 + m)], writes=["y_a"])

        def gla(l, t, pre):
            import os
            GS = int(os.environ.get("K_GSTOP", "99"))
            fence(["gtmp", "z"], GLA_KEYS)
            for k in range(KD):
                S.op("pe", lambda: nc.tensor.matmul(ps[6][0:c.R, 0:T], lhsT=walow[:, k * c.R:(k + 1) * c.R],
                                                    rhs=hbuf[:, k, :], start=(k == 0), stop=(k == KD - 1)),
                     reads=["walow", "h"], writes=[("ps", 6)], signal=(k == KD - 1))
            S.op("act", lambda: nc.scalar.copy(alT[0:c.R, :], ps[6][0:c.R, 0:T]), reads=[("ps", 6)], writes=["alT"])
            S.op("dve", lambda: nc.vector.tensor_copy(alH[:, :], alT[:, :]), reads=["alT"], writes=["alH"])
            S.op("dve", lambda: nc.vector.tensor_tensor(alL[:, :], alT[:, :], alH[:, :], ALU.subtract),
                 reads=["alT", "alH"], writes=["alL"])
            if GS == 1:
                return
            for hp in range(H // 2):
                for s in range(NS):
                    for j3, (aa, ww) in enumerate(((alH, waH), (alL, waH), (alH, waL))):
                        S.op("pe", lambda: nc.tensor.matmul(
                            ps[4 + s // 2][:, (s % 2) * 256:(s % 2) * 256 + 256], lhsT=aa[0:R1, s * 128:(s + 1) * 128],
                            rhs=ww[0:R1, hp * 256:(hp + 1) * 256], start=(j3 == 0), stop=(j3 == 2)),
                            reads=["alH", "alL", "waH", "waL"], writes=[("ps", 4 + s // 2)], signal=(j3 == 2))
                for b2 in range(NS // 2):
                    S.op("act", lambda: nc.scalar.activation(
                        cbuf[:, 2 * b2:2 * b2 + 2, :], ps[4 + b2][:, :].rearrange("p (a b) -> p a b", a=2), AF.Exp, scale=-1.0),
                        reads=[("ps", 4 + b2)], writes=["cbuf"])
                S.op("act", lambda: nc.scalar.activation(cbuf[:, :, :], cbuf[:, :, :], AF.Ln, bias=oneb[:, 0:1]),
                     reads=["cbuf", "oneb"], writes=["cbuf"])
                S.op("dve", lambda: nc.vector.tensor_copy(cH[:, :, :], cbuf[:, :, :]), reads=["cbuf"], writes=["cH"])
                S.op("dve", lambda: nc.vector.tensor_tensor(cL[:, :, :], cbuf[:, :, :], cH[:, :, :], ALU.subtract),
                     reads=["cbuf", "cH"], writes=["cL"])
                if GS == 2:
                    continue
                for s in range(NS):
                    for j2, cc_ in enumerate((cH, cL)):
                        S.op("pe", lambda: nc.tensor.matmul(
                            ps[4 + s // 2][:, (s % 2) * 256:(s % 2) * 256 + 256], lhsT=M2, rhs=cc_[:, s, :],
                            start=(j2 == 0), stop=(j2 == 1)), reads=["cfb", "cH", "cL"], writes=[("ps", 4 + s // 2)],
                            signal=(j2 == 1))
                for b2 in range(NS // 2):
                    S.op("act", lambda: nc.scalar.activation(
                        Eout[:, 2 * b2:2 * b2 + 2, :], ps[4 + b2][:, :].rearrange("p (a b) -> p a b", a=2), AF.Exp),
                        reads=[("ps", 4 + b2)], writes=["Eout"])
                for hx in range(2):
                    for s in range(NS):
                        for j2, cc_ in enumerate((cH, cL)):
                            S.op("pe", lambda: nc.tensor.matmul(
                                ps[4 + hx][:, s * 128:(s + 1) * 128], lhsT=cc_[:, s, hx * 128:(hx + 1) * 128], rhs=TriU,
                                start=(j2 == 0), stop=(j2 == 1)), reads=["cH", "cL", "cfb"], writes=[("ps", 4 + hx)],
                                signal=(j2 == 1))
                    S.op("act", lambda: nc.scalar.activation(Ein[:, hx, :], ps[4 + hx][:, 0:T], AF.Exp),
                         reads=[("ps", 4 + hx)], writes=["Ein"])
                    if not pre:
                        S.op("act", lambda: nc.scalar.activation(Einv[:, hx, :], ps[4 + hx][:, 0:T], AF.Exp, scale=-1.0),
                             reads=[("ps", 4 + hx)], writes=["Einv"])
                if GS == 3:
                    continue
                gemm(l, "k", hp, hfun, ["h"], (0, 1))
                for hx in range(2):
                    if not pre:
                        S.op("dve", lambda: nc.vector.tensor_tensor(k_inT[:, hx, :], ps[hx][:, 0:T], Einv[:, hx, :], ALU.mult),
                             reads=[("ps", hx), "Einv"], writes=["k_inT"])
                    S.op("act", lambda: nc.scalar.copy(kT_raw[:, hx, :], ps[hx][:, 0:T]),
                         reads=[("ps", hx)] + ([] if pre else ["k_inT"]), writes=["kT_raw"])
                if GS == 4:
                    continue
                pkt = ps[6][:, :].bitcast(BF16).rearrange("p (s d) -> p s d", s=NS)
                for s in range(NS):
                    for hx in range(2):
                        S.op("pe", lambda: nc.tensor.transpose(
                            pkt[:, s, hx * 128:(hx + 1) * 128], kT_raw[:, hx, s * 128:(s + 1) * 128], ident),
                            reads=["kT_raw", "cb"], writes=[("ps", 6)])
                S.op("dve", lambda: nc.vector.tensor_tensor(k_out[:, :, :], pkt, Eout[:, :, :], ALU.mult),
                     reads=[("ps", 6), "Eout"], writes=["k_out"])
                if GS == 5:
                    continue
                if not pre:
                    gemm(l, "q", hp, hfun, ["h"], (2, 3))
                    for hx in range(2):
                        S.op("dve", lambda: nc.vector.scalar_tensor_tensor(
                            q_inT[:, hx, :], ps[2 + hx][:, 0:T], float(c.DK) ** -0.5, Ein[:, hx, :], ALU.mult, ALU.mult),
                            reads=[("ps", 2 + hx), "Ein"], writes=["q_inT"])
                for hx in range(2):
                    pname, b, KC = blkpos[("v", 2 * hp + hx)]
                    vb = (0, 1) if hx == 0 else (2, 3)
                    grp = kgroups(KC)
                    svs = WR.get_many([((l, pname), b, k0, kcnt, KC) for (k0, kcnt) in grp])
                    for s in range(NS):
                        for gi, (k0, kcnt) in enumerate(grp):
                            sv = svs[gi]
                            for k in range(kcnt):
                                S.op("pe", lambda: nc.tensor.matmul(
                                    ps[vb[s // 2]][:, (s % 2) * 256:(s % 2) * 256 + 256],
                                    lhsT=hbuf[:, k0 + k, s * 128:(s + 1) * 128], rhs=wslot[sv][:, k, :],
                                    start=(gi == 0 and k == 0), stop=(gi == len(grp) - 1 and k == kcnt - 1)),
                                    reads=[("wslot", sv), "h"], writes=[("ps", vb[s // 2])],
                                    signal=(k == kcnt - 1))
                    for b2 in range(NS // 2):
                        S.op("act", lambda: nc.scalar.copy(
                            v_tok[:, 2 * b2:2 * b2 + 2, hx * 256:(hx + 1) * 256],
                            ps[vb[b2]][:, :].rearrange("p (a b) -> p a b", a=2)),
                            reads=[("ps", vb[b2])], writes=["v_tok"])
                if GS == 6:
                    continue
                if not pre:
                    for hx in range(2):
                        gemm(l, "go", 2 * hp + hx, hfun, ["h"], (0, 1))
                        for vc in range(2):
                            S.op("act", lambda: nc.scalar.activation(sgb[:, hx * 2 + vc, :], ps[vc][:, 0:T], AF.Silu),
                                 reads=[("ps", vc)], writes=["sgb"])
                for hx in range(2):
                    hi = 2 * hp + hx
                    for s in range(NS):
                        sl = slice(s * 128, (s + 1) * 128)
                        if not pre:
                            S.op("pe", lambda: nc.tensor.matmul(
                                ps[5][:, 0:128], lhsT=k_inT[:, hx, sl], rhs=q_inT[:, hx, sl], start=True, stop=True),
                                reads=["k_inT", "q_inT"], writes=[("ps", 5)])
                            S.op("dve", lambda: nc.vector.tensor_tensor(PT[s % 2][:, :], ps[5][:, 0:128], mask01, ALU.mult),
                                 reads=[("ps", 5), "cf"], writes=[("PT", s % 2)])
                            for vc in range(2):
                                S.op("pe", lambda: nc.tensor.matmul(
                                    ps[2 + vc][:, sl], lhsT=v_tok[:, s, hx * 256 + vc * 128:hx * 256 + vc * 128 + 128],
                                    rhs=PT[s % 2][:, :], start=True, stop=False),
                                    reads=["v_tok", ("PT", s % 2)], writes=[("ps", 2 + vc)], signal=False)
                                S.op("pe", lambda: nc.tensor.matmul(
                                    ps[2 + vc][:, sl], lhsT=Sbf[:, hi, vc * 128:(vc + 1) * 128],
                                    rhs=q_inT[:, hx, sl], start=False, stop=True),
                                    reads=[("Sbf", hi), "q_inT"], writes=[("ps", 2 + vc)])
                        S.op("pe", lambda: nc.tensor.matmul(
                            ps[4][:, 0:256], lhsT=k_out[:, s, hx * 128:(hx + 1) * 128],
                            rhs=v_tok[:, s, hx * 256:(hx + 1) * 256], start=True, stop=True),
                            reads=["k_out", "v_tok"], writes=[("ps", 4)])
                        dec = Ein[:, hx, s * 128 + 127:s * 128 + 128]
                        S.op("dve", lambda: nc.vector.scalar_tensor_tensor(
                            Sst[:, hi, :], Sst[:, hi, :], dec, ps[4][:, 0:256], ALU.mult, ALU.add),
                            reads=[("S", hi), "Ein", ("ps", 4)], writes=[("S", hi)])
                        if pre:
                            S.op("dve", lambda: nc.vector.tensor_tensor(
                                Dtot[:, hi:hi + 1], Dtot[:, hi:hi + 1], dec, ALU.mult),
                                reads=["Dtot", "Ein"], writes=["Dtot"])
                        else:
                            S.op("act", lambda: nc.scalar.copy(Sbf[:, hi, :], Sst[:, hi, :]),
                                 reads=[("S", hi)], writes=[("Sbf", hi)])
                    if not pre:
                        for vc in range(2):
                            S.op("act", lambda: nc.scalar.activation(sqb[vc][:, :], ps[2 + vc][:, 0:T], AF.Square),
                                 reads=[("ps", 2 + vc)], writes=[("sqb", vc)])
                            S.op("pe", lambda: nc.tensor.matmul(ps[7][:, 0:T], lhsT=onesV[:, :], rhs=sqb[vc][:, :],
                                                                start=(vc == 0), stop=(vc == 1)),
                                 reads=[("sqb", vc), "onesV"], writes=[("ps", 7)])
                        S.op("act", lambda: nc.scalar.activation(rstd[:, :], ps[7][:, 0:T], AF.Sqrt, bias=epsb[:, 0:1]),
                             reads=[("ps", 7), "epsb"], writes=["rstd"])
                        S.op("dve", lambda: nc.vector.reciprocal(rstd[:, :], rstd[:, :]),
                             reads=["rstd"], writes=["rstd"])
                        for vc in range(2):
                            S.op("dve", lambda: nc.vector.scalar_tensor_tensor(
                                tmpb[vc][:, :], ps[2 + vc][:, 0:T], P("gn", hi * 2 + vc, 1), rstd[:, :], ALU.mult, ALU.mult),
                                reads=[("ps", 2 + vc), "rstd", "params"], writes=[("tmpb", vc)])
                            S.op("dve", lambda: nc.vector.tensor_tensor(
                                y_b[:, hi * 2 + vc, :], tmpb[vc][:, :], sgb[:, hx * 2 + vc, :], ALU.mult),
                                reads=[("tmpb", vc), "sgb"], writes=["y_b"])

        def resid_gemm(l, nm, rhs_fn, rkeys, t, src_t, dst_t, capture):
            for ob in range(KD // 2):
                banks = ((0, 1), (2, 3))[ob % 2]
                gemm(l, nm, ob, rhs_fn, rkeys, banks)
                for m in range(2):
                    kc = ob * 2 + m
                    xs = XR.get((src_t, kc, t))

                    def comp(o_t, okey):
                        S.op("dve", lambda: nc.vector.tensor_tensor(o_t[:, :], ps[banks[m]][:, 0:T], xslot[xs][:, :], ALU.add),
                             reads=[("ps", banks[m]), ("xslot", xs)], writes=[okey])
                        if capture:
                            S.op("dve", lambda: nc.vector.tensor_copy(hstage[:, kc, :], o_t[:, T - 2:T]),
                                 reads=[okey], writes=["hstage"])
                    store_chunk(dst_t, kc, t, comp)

        def merge_out(l, t, src_t, dst_t, capture):
            fence(GLA_KEYS + ["gtmp"], ["z"])
            for ob in range(KD // 2):
                gemm(l, "co", ob, lambda k: y_a[:, k, :], ["y_a"], (0, 1))
                gemm(l, "ga", ob, hfun, ["h"], (2, 3))
                for m in range(2):
                    S.op("act", lambda: nc.scalar.activation(tmpb[m][:, :], ps[2 + m][:, 0:T], AF.Sigmoid,
                                                             bias=P("bg", ob * 2 + m, 1)),
                         reads=[("ps", 2 + m), "params"], writes=[("tmpb", m)])
                    S.op("dve", lambda: nc.vector.tensor_tensor(accb[m][:, 0:T], ps[m][:, 0:T], tmpb[m][:, :], ALU.mult),
                         reads=[("ps", m), ("tmpb", m)], writes=[("acc", m)])
                gemm(l, "gl", ob, lambda k: y_b[:, k, :], ["y_b"], (4, 5))
                gemm(l, "gb", ob, hfun, ["h"], (6, 7))
                for m in range(2):
                    S.op("act", lambda: nc.scalar.activation(tmpb[m][:, :], ps[6 + m][:, 0:T], AF.Sigmoid,
                                                             bias=P("bg", KD + ob * 2 + m, 1)),
                         reads=[("ps", 6 + m), "params"], writes=[("tmpb", m)])
                    S.op("dve", lambda: nc.vector.tensor_tensor(accb[2 + m][:, 0:T], ps[4 + m][:, 0:T], tmpb[m][:, :], ALU.mult),
                         reads=[("ps", 4 + m), ("tmpb", m)], writes=[("acc", 2 + m)])
                    S.op("dve", lambda: nc.vector.tensor_tensor(zbuf[:, ob * 2 + m, :], accb[2 + m][:, 0:T], accb[m][:, 0:T], ALU.add),
                         reads=[("acc", 2 + m), ("acc", m)], writes=["z"])
            resid_gemm(l, "wo", lambda k: zbuf[:, k, :], ["z"], t, src_t, dst_t, capture)

        def ffn(l, t, src_t, dst_t):
            import os
            FS = int(os.environ.get("K_FSTOP", "99"))
            fence(["y_a", "y_b", "z", "gtmp"] + GLA_KEYS + S_KEYS, ["a"])
            for f in range(FC):
                banks = ((0, 1), (2, 3))[f % 2]
                gemm(l, "up", f, hfun, ["h"], banks, extra=halo_mm(0) if t == 0 else None)
                if t == 0:
                    for m in range(2):
                        S.op("act", lambda: nc.scalar.copy(hsm[:, m * 2:m * 2 + 2], ps[6 + m][:, 0:2]),
                             reads=[("ps", 6 + m)], writes=["hsm"])
                for m in range(2):
                    ch = m * FC + f
                    if t == 0:
                        halo, hk = hsm[:, m * 2:m * 2 + 2], ["hsm"]
                    else:
                        S.op("dve", lambda: nc.vector.tensor_copy(hsm[:, 4 + m * 2:6 + m * 2], fh_save[:, ch, :]),
                             reads=["fh_save"], writes=[("hsmb", m)])
                        halo, hk = hsm[:, 4 + m * 2:6 + m * 2], [("hsmb", m)]
                    conv_from(accb[m], ("acc", m), ps[banks[m]][:, 0:T], [("ps", banks[m])], halo, hk,
                              P("fw", ch * 3 + 0, 1), P("fw", ch * 3 + 1, 1), P("fw", ch * 3 + 2, 1))
                    S.op("act", lambda: nc.scalar.copy(fh_save[:, ch, :], ps[banks[m]][:, T - 2:T]),
                         reads=[("ps", banks[m])], writes=["fh_save"])
                S.op("act", lambda: nc.scalar.activation(tmpb[0][:, :], accb[0][:, 0:T], AF.Silu),
                     reads=[("acc", 0)], writes=[("tmpb", 0)])
                S.op("dve", lambda: nc.vector.tensor_tensor(abuf[:, f, :], tmpb[0][:, :], accb[1][:, 0:T], ALU.mult),
                     reads=[("tmpb", 0), ("acc", 1)], writes=["a"])
            if FS == 1:
                return
            resid_gemm(l, "dn", lambda k: abuf[:, k, :], ["a"], t, src_t, dst_t, False)

        def fold_state():
            fence(GLA_KEYS + ["z"], ["gtmp"])
            S.op("dve", lambda: nc.vector.memset(Sst[:, :, :], 0.0), writes=[("S", hi) for hi in range(H)])
            for q in range(NG - 1 if c.use_cc else 0):
                S.dma("pool", "gq", serial=True, fn=lambda: nc.gpsimd.dma_start(out=gtmp[:, :], in_=xg_dst.ap()[q * 128:(q + 1) * 128, :]),
                      reads=["xg_dst"], writes=["gtmp"])
                mq = cm[:, q:q + 1]
                S.op("dve", lambda: nc.vector.tensor_scalar(atmp[:, :], gtmp[:, H * 256:H * 256 + H], -1.0, mq, ALU.add, ALU.mult),
                     reads=["gtmp", "cm"], writes=["atmp"])
                S.op("dve", lambda: nc.vector.tensor_scalar(atmp[:, :], atmp[:, :], 1.0, None, ALU.add),
                     reads=["atmp"], writes=["atmp"])
                S.op("dve", lambda: nc.vector.tensor_scalar(gtmp[:, 0:H * 256], gtmp[:, 0:H * 256], mq, None, ALU.mult),
                     reads=["gtmp", "cm"], writes=["gtmp"])
                for hi in range(H):
                    S.op("dve", lambda: nc.vector.scalar_tensor_tensor(
                        Sst[:, hi, :], Sst[:, hi, :], atmp[:, hi:hi + 1], gtmp[:, hi * 256:(hi + 1) * 256], ALU.mult, ALU.add),
                        reads=[("S", hi), "atmp", "gtmp"], writes=[("S", hi)])
            for hi in range(H):
                S.op("act", lambda: nc.scalar.copy(Sbf[:, hi, :], Sst[:, hi, :]), reads=[("S", hi)], writes=[("Sbf", hi)])

        import os
        STOP = int(os.environ.get("K_STOP", "99"))
        if STOP >= 1:
            gather_weights(0)
        cur = xT_in
        for l in range(L if STOP >= 2 else 0):
            nxt = xlay[l]
            load_params(l)
            if STOP == 2:
                break
            fence(["a"], ["y_a", "y_b", "z", "gtmp"] + GLA_KEYS + S_KEYS)
            S.op("dve", lambda: nc.vector.memset(Sst[:, :, :], 0.0), writes=[("S", hi) for hi in range(H)])
            S.op("dve", lambda: nc.vector.memset(Dtot[:, :], 1.0), writes=["Dtot"])
            for t in range(c.NT):
                norm_tile(cur, t, "g1", capture=(t == c.NT - 1))
                if STOP == 21:
                    continue
                gla(l, t, pre=True)
            if STOP == 3:
                break
            exchange(xh_src, xh_dst, hstage[:, :, :].rearrange("p a b -> p (a b)"), ["hstage"], "xh_dst")
            if STOP == 4:
                break
            fence(GLA_KEYS + ["z"], ["gtmp"])
            S.op("dve", lambda: nc.vector.tensor_copy(gtmp[:, 0:H * 256], Sst[:, :, :].rearrange("p a b -> p (a b)")),
                 reads=[("S", hi) for hi in range(H)], writes=["gtmp"])
            S.op("dve", lambda: nc.vector.tensor_copy(gtmp[:, H * 256:H * 256 + H], Dtot[:, :]), reads=["Dtot"], writes=["gtmp"])
            exchange(xg_src, xg_dst, gtmp[:, :], ["gtmp"], "xg_dst")
            if STOP == 5:
                break
            for t in range(c.NT):
                norm_tile(cur, t, "g1")
                if t == 0:
                    halo_tile("g1")
                if STOP == 6:
                    continue
                mixer_a(l, t)
                if STOP == 7:
                    continue
                if t == 0:
                    fold_state()
                if STOP == 8:
                    continue
                gla(l, t, pre=False)
                if STOP == 9:
                    continue
                merge_out(l, t, cur, xmid, capture=(t == c.NT - 1))
            if STOP <= 10:
                break
            exchange(xh_src, xh_dst, hstage[:, :, :].rearrange("p a b -> p (a b)"), ["hstage"], "xh_dst")
            if l + 1 < L:
                gather_weights(l + 1)
            for t in range(c.NT):
                norm_tile(xmid, t, "g2")
                if t == 0:
                    halo_tile("g2")
                if STOP == 11:
                    continue
                ffn(l, t, xmid, nxt)
            if STOP <= 12:
                break
            cur = nxt
        for t in range(c.NT):
            sumsq(cur, t)
            for kc in range(KD):
                xs = XR.get((cur, kc, t))

                def comp(o_t, okey):
                    S.op("dve", lambda: nc.vector.scalar_tensor_tensor(
                        o_t[:, :], xslot[xs][:, :], gf[:, kc:kc + 1], rstd[:, :], ALU.mult, ALU.mult),
                        reads=[("xslot", xs), "rstd", "gf"], writes=[okey])
                store_chunk(outT, kc, t, comp)
        for i in range(NO):
            if S.cnt[f"o{i}"] > 0 and S.active == "act":
                nc.scalar.wait_ge(sems[f"o{i}"], S.cnt[f"o{i}"])
        if S.active == "pool":
            for sn in ("gq", "cc"):
                if S.cnt[sn] > 0:
                    nc.gpsimd.wait_ge(sems[sn], S.cnt[sn])
        if S.active == "sp":
            for sn in ["ld"] + [f"w{i}" for i in range(NW)] + [f"x{i}" for i in range(NX)]:
                if S.cnt[sn] > 0:
                    nc.sync.wait_ge(sems[sn], S.cnt[sn])

    import os, time as _time
    NONCE = int(os.environ.get("K_NONCE", "0"))
    S.reset(None)
    program()
    stats = dict(S.nins)

    def run_pass(name):
        def body(_e):
            S.reset(name)
            WR.restart()
            XR.restart()
            program()
        return body

    block.tensor(run_pass("pe"))
    block.scalar(run_pass("act"))
    block.vector(run_pass("dve"))
    block.gpsimd(run_pass("pool"))
    block.sync(run_pass("sp"))
    es.close()
    return nc, stats


_CACHE = {}


def run(cfg, inputs, trace=False):
    if id(cfg) not in _CACHE:
        _CACHE[id(cfg)] = build_program(cfg)[0]
    nc = _CACHE[id(cfg)]
    maps = make_in_maps(cfg, inputs)
    res = run_bass_kernel_spmd(nc, maps, core_ids=list(range(cfg.NCORE)), **({"trace": True} if trace else {}))
    out = np.empty((cfg.BATCH, cfg.SEQ, cfg.D), np.float32)
    for r in range(cfg.NCORE):
        b, p = r // cfg.NG, r % cfg.NG
        out[b, p * cfg.TC:(p + 1) * cfg.TC, :] = res.results[r]["outT"].T
    if trace:
        return out, res
    return out


def kernel(**inputs):
    return run(FULL, inputs)
```
